# Optimizing a Trainium2 kernel written in Bass

```python
import math
import jax, jax.numpy as jnp
from jax import lax
import numpy as np

D_MODEL = 1024
BATCH = 2
SEQ = 8192
DEPTH = 4

HEAD_DIM = 64
A_HEADS = 4
A_CONFIGS = ((128, 1), (512, 4), (2048, 16))
A_ROPE_DIMS = HEAD_DIM // 4
ROPE_THETA = 500000.0
B_Q_HEADS = 8
B_KV_HEADS = 2
B_AXIAL_THETA = 10000.0
C_HEADS = 4
C_ROWS_MAX = 8
C_COLS = 16
GRID_W = 64
Q_BLOCK = 128
D_FF = 4 * D_MODEL
A_WIDTH = A_HEADS * HEAD_DIM
B_Q_WIDTH = B_Q_HEADS * HEAD_DIM
B_KV_WIDTH = B_KV_HEADS * HEAD_DIM
C_WIDTH = C_HEADS * HEAD_DIM
N_BRANCHES = 3
QKV_COLS = 3 * A_WIDTH + B_Q_WIDTH + 2 * B_KV_WIDTH + 3 * C_WIDTH
IN_COLS = QKV_COLS + N_BRANCHES * D_MODEL
DEEPNORM_ALPHA = (2 * DEPTH) ** 0.25
DEEPNORM_BETA = (8 * DEPTH) ** -0.25
LN_EPS = 1e-5
RMS_EPS = 1e-6
NEG_INF = -1e30

kernel_name = "hybrid_dilated_axial_neighbourhood_encoder"


def _split_points():
    sizes = [A_WIDTH, A_WIDTH, A_WIDTH, B_Q_WIDTH, B_KV_WIDTH, B_KV_WIDTH,
             C_WIDTH, C_WIDTH, C_WIDTH]
    return tuple(int(v) for v in np.cumsum(sizes))


def layer_norm(x, g, b):
    xf = x.astype(jnp.float32)
    mu = jnp.mean(xf, -1, keepdims=True)
    var = jnp.mean(jnp.square(xf - mu), -1, keepdims=True)
    y = (xf - mu) * lax.rsqrt(var + LN_EPS)
    return (y * g.astype(jnp.float32) + b.astype(jnp.float32)).astype(x.dtype)


def rms_norm(x, g):
    xf = x.astype(jnp.float32)
    y = xf * lax.rsqrt(jnp.mean(jnp.square(xf), -1, keepdims=True) + RMS_EPS)
    return (y * g.astype(jnp.float32)).astype(x.dtype)


def rotary(x, pos, theta):
    half = x.shape[-1] // 2
    inv = theta ** (-jnp.arange(half, dtype=jnp.float32) / half)
    ang = pos.astype(jnp.float32)[:, None] * inv[None, :]
    cos = jnp.cos(ang)[None, :, None, :]
    sin = jnp.sin(ang)[None, :, None, :]
    xf = x.astype(jnp.float32)
    x1, x2 = xf[..., :half], xf[..., half:]
    return jnp.concatenate([x1 * cos - x2 * sin, x2 * cos + x1 * sin], -1).astype(x.dtype)


def partial_rotary(x, pos):
    return jnp.concatenate([rotary(x[..., :A_ROPE_DIMS], pos, ROPE_THETA),
                            x[..., A_ROPE_DIMS:]], -1)


def axial_rotary(x, row, col):
    half = x.shape[-1] // 2
    return jnp.concatenate([rotary(x[..., :half], row, B_AXIAL_THETA),
                            rotary(x[..., half:], col, B_AXIAL_THETA)], -1)


def banded_window_stats(q, k, v, radius):
    L, hd = q.shape[-2], q.shape[-1]
    lead = q.shape[:-2]
    nb = -(-L // Q_BLOCK)
    lp = nb * Q_BLOCK
    pad_q = [(0, 0)] * len(lead) + [(0, lp - L), (0, 0)]
    pad_kv = [(0, 0)] * len(lead) + [(radius, lp - L + radius), (0, 0)]
    qb = jnp.pad(q, pad_q).reshape(lead + (nb, Q_BLOCK, hd))
    kp = jnp.pad(k, pad_kv)
    vp = jnp.pad(v, pad_kv)
    span = Q_BLOCK + 2 * radius
    idx = jnp.arange(nb)[:, None] * Q_BLOCK + jnp.arange(span)[None, :]
    kb = kp[..., idx, :]
    vb = vp[..., idx, :].astype(jnp.float32)
    qi = jnp.arange(lp).reshape(nb, Q_BLOCK)[:, :, None]
    kj = (idx - radius)[:, None, :]
    mask = (jnp.abs(qi - kj) <= radius) & (kj >= 0) & (kj < L)
    s = jnp.einsum('...nqd,...nkd->...nqk', qb, kb, preferred_element_type=jnp.float32)
    s = jnp.where(mask, s, NEG_INF)
    m = jnp.max(s, -1)
    p = jnp.exp(s - m[..., None])
    l = jnp.sum(p, -1)
    o = jnp.einsum('...nqk,...nkd->...nqd', p, vb)
    m = m.reshape(lead + (lp,))[..., :L]
    l = l.reshape(lead + (lp,))[..., :L]
    o = o.reshape(lead + (lp, hd))[..., :L, :]
    return m, l, o


def dilated_attention(q, k, v):
    b, s, h, hd = q.shape
    ms, ls, outs = [], [], []
    for window, dil in A_CONFIGS:
        radius = window // (2 * dil)
        L = s // dil

        def to_sub(t):
            return t.reshape(b, L, dil, h, hd).transpose(0, 2, 3, 1, 4)

        m, l, o = banded_window_stats(to_sub(q), to_sub(k), to_sub(v), radius)
        ms.append(m.transpose(0, 3, 1, 2).reshape(b, s, h))
        ls.append(l.transpose(0, 3, 1, 2).reshape(b, s, h))
        outs.append(o.transpose(0, 3, 1, 2, 4).reshape(b, s, h, hd))
    m_all = jnp.stack(ms)
    l_all = jnp.stack(ls)
    o_all = jnp.stack(outs)
    m_max = jnp.max(m_all, 0)
    w = jnp.exp(m_all - m_max)
    out = jnp.sum(w[..., None] * o_all, 0) / jnp.sum(w * l_all, 0)[..., None]
    return out.astype(q.dtype)


def axial_gqa(q, k, v):
    b, s, hq, hd = q.shape
    hkv = k.shape[2]
    g = hq // hkv
    nq = s // Q_BLOCK
    qb = q.reshape(b, nq, Q_BLOCK, hkv, g, hd).transpose(1, 0, 2, 3, 4, 5)

    def block(qblk):
        sc = jnp.einsum('bqhgd,bkhd->bhgqk', qblk, k, preferred_element_type=jnp.float32)
        p = jax.nn.softmax(sc, -1)
        return jnp.einsum('bhgqk,bkhd->bqhgd', p.astype(v.dtype), v)

    o = lax.map(block, qb)
    return o.transpose(1, 0, 2, 3, 4, 5).reshape(b, s, hq * hd)


def neighbourhood_attention(q, k, v, rpb):
    b, s, h, hd = q.shape
    rows = s // GRID_W
    kr = min(C_ROWS_MAX, rows)

    def grid(t):
        return t.reshape(b, rows, GRID_W, h, hd).transpose(0, 3, 1, 2, 4)

    qg, kg, vg = grid(q), grid(k), grid(v)
    r = jnp.arange(rows)
    r0 = jnp.clip(r - kr // 2, 0, rows - kr)
    row_idx = r0[:, None] + jnp.arange(kr)[None, :]
    kn = kg[:, :, row_idx]
    vn = vg[:, :, row_idx]
    c = jnp.arange(GRID_W)
    c0 = jnp.clip(c - C_COLS // 2, 0, GRID_W - C_COLS)
    col_mask = (c[None, :] >= c0[:, None]) & (c[None, :] < c0[:, None] + C_COLS)
    dr = row_idx - r[:, None] + (C_ROWS_MAX - 1)
    dc = jnp.clip(c[None, :] - c[:, None] + (C_COLS - 1), 0, 2 * C_COLS - 2)
    bias = rpb[:, dr[:, None, :, None], dc[None, :, None, :]]
    sc = jnp.einsum('bhrqd,bhrikd->bhrqik', qg, kn, preferred_element_type=jnp.float32)
    sc = sc + bias.astype(jnp.float32)
    sc = jnp.where(col_mask[:, None, :], sc, NEG_INF)
    p = jax.nn.softmax(sc.reshape(b, h, rows, GRID_W, kr * GRID_W), -1)
    p = p.reshape(b, h, rows, GRID_W, kr, GRID_W)
    o = jnp.einsum('bhrqik,bhrikd->bhrqd', p.astype(v.dtype), vn)
    return o.transpose(0, 2, 3, 1, 4).reshape(b, s, h * hd)


def setup_inputs(seed: int = 0) -> dict:
    key = jax.random.key(seed)
    ks = jax.random.split(key, 16)
    f32 = jnp.float32

    def normal(k, shape, scale):
        return jax.random.normal(k, shape, f32) * scale

    x = normal(ks[0], (BATCH, SEQ, D_MODEL), 1.0)
    w_in = normal(ks[1], (DEPTH, D_MODEL, IN_COLS), D_MODEL ** -0.5)
    b_gate = normal(ks[2], (DEPTH, N_BRANCHES * D_MODEL), 0.02)
    q_norm_b = 1.0 + normal(ks[3], (DEPTH, HEAD_DIM), 0.02)
    k_norm_b = 1.0 + normal(ks[4], (DEPTH, HEAD_DIM), 0.02)
    rpb_c = normal(ks[5], (DEPTH, C_HEADS, 2 * C_ROWS_MAX - 1, 2 * C_COLS - 1), 0.1)
    w_branch_a = normal(ks[6], (DEPTH, A_WIDTH, D_MODEL), A_WIDTH ** -0.5 * DEEPNORM_BETA)
    w_branch_b = normal(ks[7], (DEPTH, B_Q_WIDTH, D_MODEL), B_Q_WIDTH ** -0.5 * DEEPNORM_BETA)
    w_branch_c = normal(ks[8], (DEPTH, C_WIDTH, D_MODEL), C_WIDTH ** -0.5 * DEEPNORM_BETA)
    w_out = normal(ks[9], (DEPTH, D_MODEL, D_MODEL), D_MODEL ** -0.5 * DEEPNORM_BETA)
    ln1_g = 1.0 + normal(ks[10], (DEPTH, D_MODEL), 0.02)
    ln1_b = normal(ks[11], (DEPTH, D_MODEL), 0.02)
    w_up = normal(ks[12], (DEPTH, D_MODEL, D_FF), D_MODEL ** -0.5)
    w_down = normal(ks[13], (DEPTH, D_FF, D_MODEL), D_FF ** -0.5 * DEEPNORM_BETA)
    ln2_g = 1.0 + normal(ks[14], (DEPTH, D_MODEL), 0.02)
    ln2_b = normal(ks[15], (DEPTH, D_MODEL), 0.02)
    return {"x": x, "w_in": w_in, "b_gate": b_gate, "q_norm_b": q_norm_b,
            "k_norm_b": k_norm_b, "rpb_c": rpb_c, "w_branch_a": w_branch_a,
            "w_branch_b": w_branch_b, "w_branch_c": w_branch_c, "w_out": w_out,
            "ln1_g": ln1_g, "ln1_b": ln1_b, "w_up": w_up, "w_down": w_down,
            "ln2_g": ln2_g, "ln2_b": ln2_b}


def reference(x, w_in, b_gate, q_norm_b, k_norm_b, rpb_c, w_branch_a, w_branch_b,
              w_branch_c, w_out, ln1_g, ln1_b, w_up, w_down, ln2_g, ln2_b):
    b, s, _ = x.shape
    pos = jnp.arange(s)
    row = pos // GRID_W
    col = pos % GRID_W
    scale = HEAD_DIM ** -0.5
    splits = _split_points()

    def heads(t, n):
        return t.reshape(b, s, n, HEAD_DIM)

    for layer in range(DEPTH):
        h = x @ w_in[layer]
        qa, ka, va, qb, kb, vb, qc, kc, vc, gate_logits = jnp.split(h, splits, axis=-1)

        qa = partial_rotary(heads(qa, A_HEADS), pos) * scale
        ka = partial_rotary(heads(ka, A_HEADS), pos)
        oa = dilated_attention(qa, ka, heads(va, A_HEADS)).reshape(b, s, A_WIDTH)

        qb = axial_rotary(rms_norm(heads(qb, B_Q_HEADS), q_norm_b[layer]), row, col) * scale
        kb = axial_rotary(rms_norm(heads(kb, B_KV_HEADS), k_norm_b[layer]), row, col)
        ob = axial_gqa(qb, kb, heads(vb, B_KV_HEADS))

        oc = neighbourhood_attention(heads(qc, C_HEADS) * scale, heads(kc, C_HEADS),
                                     heads(vc, C_HEADS), rpb_c[layer])

        g = jax.nn.sigmoid((gate_logits + b_gate[layer]).astype(jnp.float32)).astype(x.dtype)
        g = g.reshape(b, s, N_BRANCHES, D_MODEL)
        merged = (g[:, :, 0] * (oa @ w_branch_a[layer])
                  + g[:, :, 1] * (ob @ w_branch_b[layer])
                  + g[:, :, 2] * (oc @ w_branch_c[layer]))
        mix = merged @ w_out[layer]
        x = layer_norm(DEEPNORM_ALPHA * x + mix, ln1_g[layer], ln1_b[layer])

        ff = jnp.square(jax.nn.relu(x @ w_up[layer])) @ w_down[layer]
        x = layer_norm(DEEPNORM_ALPHA * x + ff, ln2_g[layer], ln2_b[layer])
    return x
```

```python
import math
from contextlib import ExitStack

import numpy as np
import ml_dtypes
import concourse.bass as bass
import concourse.mybir as mybir
from concourse.bass_utils import run_bass_kernel_spmd

F32 = mybir.dt.float32
BF16 = mybir.dt.bfloat16
AF = mybir.ActivationFunctionType
ALU = mybir.AluOpType
BF = ml_dtypes.bfloat16

D = 1024
S = 8192
T = 2048
NT = 16
DEPTH = 4
ALPHA = (2 * DEPTH) ** 0.25
LN_EPS = 1e-5
RMS_EPS = 1e-6
SCALE = 0.125
NCORES = 8
ENG = ["pe", "act", "dve", "pool", "sp"]
NDSEM = 6


class Prog:
    def __init__(self, nc):
        self.nc = nc
        self.ops = {e: [] for e in ENG}
        self.dkeys = {}

    def op(self, eng, fn, deps=()):
        self.ops[eng].append({"fn": fn, "deps": [d for d in deps if d is not None], "kind": "c", "marked": False})
        return ("c", eng, len(self.ops[eng]) - 1)

    def dma(self, eng, fn, deps=(), key=None):
        if key is None:
            key = ("uniq", len(self.dkeys))
        if key not in self.dkeys:
            self.dkeys[key] = 0
        self.dkeys[key] += 1
        self.ops[eng].append({"fn": fn, "deps": [d for d in deps if d is not None], "kind": "d", "sem": key})
        return ("d", key, None, 16 * self.dkeys[key])

    def wait_all(self, eng, deps):
        self.ops[eng].append({"fn": None, "deps": list(deps), "kind": "w"})

    def emit(self):
        nc = self.nc
        ops = self.ops
        for e in ENG:
            for o in ops[e]:
                for d in o["deps"]:
                    if d[0] == "c":
                        ops[d[1]][d[2]]["marked"] = True
        for e in ENG:
            c = 0
            for o in ops[e]:
                if o["kind"] == "c" and o["marked"]:
                    c += 1
                o["cum"] = c
        with ExitStack() as st:
            csem = {e: st.enter_context(nc.semaphore("cs_" + e)) for e in ENG}
            dsem = {}
            for i, k in enumerate(self.dkeys):
                dsem[k] = st.enter_context(nc.semaphore("ds_%d" % i))
            block = st.enter_context(nc.Block())

            def run(e, engine):
                waited = {}
                for o in ops[e]:
                    need = {}
                    for d in o["deps"]:
                        if d[0] == "c":
                            if d[1] == e and e == "pe":
                                continue
                            key = ("c", d[1])
                            sem = csem[d[1]]
                            val = ops[d[1]][d[2]]["cum"]
                        else:
                            key = ("d", d[1])
                            sem = dsem[d[1]]
                            val = d[3]
                        if key not in need or need[key][1] < val:
                            need[key] = (sem, val)
                    for key, (sem, val) in need.items():
                        if waited.get(key, 0) >= val:
                            continue
                        engine.wait_ge(sem, val)
                        waited[key] = val
                    if o["fn"] is None:
                        continue
                    ins = o["fn"](engine)
                    if o["kind"] == "c":
                        if o["marked"]:
                            ins.then_inc(csem[e], 1)
                    else:
                        ins.then_inc(dsem[o["sem"]], 16)

            @block.tensor
            def _(t):
                run("pe", t)

            @block.scalar
            def _(a):
                run("act", a)

            @block.vector
            def _(v):
                run("dve", v)

            @block.gpsimd
            def _(g):
                run("pool", g)

            @block.sync
            def _(s):
                run("sp", s)


class Buf:
    def __init__(self, ap=None):
        self.ap = ap
        self.w = []
        self.r = []

    def wdeps(self):
        return self.w + self.r

    def set_w(self, *toks):
        self.w = list(toks)
        self.r = []

    def add_w(self, tok):
        self.w.append(tok)

    def rdeps(self):
        return list(self.w)

    def add_r(self, tok):
        self.r.append(tok)


class Ring:
    def __init__(self, aps):
        self.bufs = [Buf(a) for a in aps]
        self.i = 0

    def next(self):
        b = self.bufs[self.i % len(self.bufs)]
        self.i += 1
        return b


class Ctx:
    def __init__(self, nc, st):
        self.nc = nc
        self.st = st
        self.n = 0

    def sb(self, shape, dt, name=None):
        self.n += 1
        return self.st.enter_context(self.nc.sbuf_tensor("s_" + (name or ("sb%d" % self.n)), list(shape), dt))

    def ps(self, shape, dt, name=None):
        self.n += 1
        return self.st.enter_context(self.nc.psum_tensor("p_" + (name or ("ps%d" % self.n)), list(shape), dt))

    def sb_ring(self, n, shape, dt):
        return Ring([self.sb(shape, dt)[:] for _ in range(n)])

    def ps_ring(self, n, shape, dt):
        return Ring([self.ps(shape, dt)[:] for _ in range(n)])


def dram_in(nc, name, shape, dt=F32):
    return nc.dram_tensor(name, list(shape), dt, kind="ExternalInput").ap()


def dram_out(nc, name, shape, dt):
    return nc.dram_tensor(name, list(shape), dt, kind="ExternalOutput").ap()


def emit_make_xT(P, C, xT, xT_buf, ident, ident_tok, get_tile, extra_wdeps=()):
    xb_ring = C.sb_ring(2, [128, D], BF16)
    tp_ring = C.ps_ring(2, [128, 8, 128], BF16)
    wd = xT_buf.wdeps() + list(extra_wdeps)
    toks = []
    for t in range(NT):
        src, rdeps, rcb = get_tile(t)
        xb = xb_ring.next()
        tk = P.op("act", lambda a, o=xb.ap, i=src: a.activation(out=o, in_=i, func=AF.Copy), rdeps + xb.wdeps())
        if rcb is not None:
            rcb(tk)
        xb.set_w(tk)
        tp = tp_ring.next()
        last = None
        for kc in range(8):
            last = P.op("pe", lambda pe, o=tp.ap[:, kc, :], i=xb.ap[:, kc * 128:(kc + 1) * 128]: pe.transpose(out=o, in_=i, identity=ident),
                        xb.rdeps() + tp.wdeps() + [ident_tok])
        xb.add_r(last)
        tp.set_w(last)
        tk2 = P.op("dve", lambda v, o=xT[:, :, t * 128:(t + 1) * 128], i=tp.ap: v.tensor_copy(out=o, in_=i), tp.rdeps() + wd)
        tp.add_r(tk2)
        toks.append(tk2)
    xT_buf.set_w(*toks)


QK_TYPES = ["A", "A", "A", "A", "Bq", "Bq", "Bq", "Bq", "Bk", "C", "C", "C", "C"]


def build_p1():
    nc = bass.Bass("TRN2", target_bir_lowering=False)
    x = dram_in(nc, "x", [T, D])
    wqk = dram_in(nc, "wqk", [D, 13 * 128])
    wv = dram_in(nc, "wv", [D, 640])
    rot = dram_in(nc, "rot", [4, 128, T])
    cst = dram_in(nc, "cst", [4, 128, 128])
    gains = dram_in(nc, "gains", [128, 2])
    qkT = dram_out(nc, "qkT", [13, 128, T], BF16)
    vtok = dram_out(nc, "vtok", [T, 640], BF16)
    P = Prog(nc)
    with ExitStack() as st:
        C = Ctx(nc, st)
        xT_t = C.sb([128, 8, T], BF16, "xT")
        xT = xT_t[:]
        xT_buf = Buf()
        cst_sb = C.sb([128, 4, 128], BF16, "cst")
        gains_sb = C.sb([128, 2], F32, "gains")
        eps_sb = C.sb([128, 1], F32, "eps")
        wv_sb = C.sb([128, 8, 640], BF16, "wv")
        ident = cst_sb[:, 0, :]
        swA = cst_sb[:, 1, :]
        swB = cst_sb[:, 2, :]
        bones = cst_sb[:, 3, :]
        cst_tok = P.dma("pool", lambda g: g.dma_start(out=cst_sb[:], in_=cst.rearrange("c p f -> p c f")))
        gains_tok = P.dma("sp", lambda s: s.dma_start(out=gains_sb[:], in_=gains[:, :]))
        eps_tok = P.op("pool", lambda g: g.memset(eps_sb[:], RMS_EPS))
        wv_tok = P.dma("pool", lambda g: g.dma_start(out=wv_sb[:], in_=wv.rearrange("(kc p) f -> p kc f", p=128)))

        xld = C.sb_ring(2, [128, D], F32)

        def get_tile(t):
            b = xld.next()
            tk = P.dma("sp", lambda s, o=b.ap, i=x[t * 128:(t + 1) * 128, :]: s.dma_start(out=o, in_=i), b.wdeps(), key=b)
            b.set_w(tk)
            return b.ap, [tk], b.add_r

        emit_make_xT(P, C, xT, xT_buf, ident, cst_tok, get_tile)

        w_ring = C.sb_ring(3, [128, 8, 128], BF16)
        acc_ring = C.ps_ring(2, [128, 512], F32)
        aux_ring = C.ps_ring(1, [128, 512], F32)
        ms_ring = C.ps_ring(1, [128, 512], F32)
        rot_ring = C.sb_ring(2, [128, 2, 512], F32)
        y_ring = C.sb_ring(2, [128, 512], BF16)
        sq_ring = C.sb_ring(2, [128, 512], BF16)
        rs_ring = C.sb_ring(2, [128, 512], F32)
        t1_ring = C.sb_ring(2, [128, 512], F32)
        y32_ring = C.sb_ring(2, [128, 512], F32)
        a32_ring = C.sb_ring(2, [128, 512], F32)
        t2_ring = C.sb_ring(2, [128, 512], F32)
        ob_ring = C.sb_ring(3, [128, 512], BF16)
        out_toks = []
        import os
        for fc in [int(v) for v in os.environ.get('P1_FC', '0,1,2,3,4,5,6,7,8,9,10,11,12').split(',') if v != '']:
            typ = QK_TYPES[fc]
            wb = w_ring.next()
            wtk = P.dma("pool", lambda g, o=wb.ap, i=wqk[:, fc * 128:(fc + 1) * 128].rearrange("(kc p) f -> p kc f", p=128): g.dma_start(out=o, in_=i), wb.wdeps(), key=wb)
            wb.set_w(wtk)
            for tb in range(4):
                ts = slice(tb * 512, (tb + 1) * 512)
                acc = acc_ring.next()
                last = None
                for kc in range(8):
                    last = P.op("pe", lambda pe, o=acc.ap, l=wb.ap[:, kc, :], r=xT[:, kc, ts], k=kc: pe.matmul(o, lhsT=l, rhs=r, start=(k == 0), stop=(k == 7)),
                                wb.rdeps() + xT_buf.rdeps() + acc.wdeps())
                wb.add_r(last)
                acc.set_w(last)
                ob = ob_ring.next()
                if typ == "C":
                    tk = P.op("act", lambda a, o=ob.ap, i=acc.ap: a.activation(out=o, in_=i, func=AF.Copy), acc.rdeps() + ob.wdeps())
                    acc.add_r(tk)
                    ob.set_w(tk)
                else:
                    isB = typ != "A"
                    rt = rot_ring.next()
                    r0 = 2 if isB else 0
                    rtk = P.dma("sp", lambda s, o=rt.ap, i=rot[r0:r0 + 2, :, ts].rearrange("c p f -> p c f"): s.dma_start(out=o, in_=i), rt.wdeps(), key=rt)
                    rt.set_w(rtk)
                    y = y_ring.next()
                    if not isB:
                        tk = P.op("act", lambda a, o=y.ap, i=acc.ap: a.activation(out=o, in_=i, func=AF.Copy), acc.rdeps() + y.wdeps())
                        acc.add_r(tk)
                        y.set_w(tk)
                        y32 = y32_ring.next()
                        tk = P.op("act", lambda a, o=y32.ap, i=acc.ap: a.activation(out=o, in_=i, func=AF.Copy), acc.rdeps() + y32.wdeps())
                        acc.add_r(tk)
                        y32.set_w(tk)
                        src_tok_holder = y32
                        src_ap = y32.ap
                    else:
                        sq = sq_ring.next()
                        tk = P.op("act", lambda a, o=sq.ap, i=acc.ap: a.activation(out=o, in_=i, func=AF.Square), acc.rdeps() + sq.wdeps())
                        acc.add_r(tk)
                        sq.set_w(tk)
                        ms = ms_ring.next()
                        tk = P.op("pe", lambda pe, o=ms.ap, r=sq.ap: pe.matmul(o, lhsT=bones, rhs=r, start=True, stop=True), sq.rdeps() + ms.wdeps() + [cst_tok])
                        sq.add_r(tk)
                        ms.set_w(tk)
                        rs = rs_ring.next()
                        tk = P.op("act", lambda a, o=rs.ap, i=ms.ap: a.activation(out=o, in_=i, func=AF.Sqrt, bias=eps_sb[:, 0:1], scale=1.0), ms.rdeps() + rs.wdeps() + [eps_tok])
                        ms.add_r(tk)
                        rs.set_w(tk)
                        tk = P.op("dve", lambda v, o=rs.ap: v.reciprocal(out=o, in_=o), rs.rdeps())
                        rs.set_w(tk)
                        gcol = 0 if typ == "Bq" else 1
                        y32 = y32_ring.next()
                        tk = P.op("act", lambda a, o=y32.ap, i=acc.ap: a.activation(out=o, in_=i, func=AF.Copy), acc.rdeps() + y32.wdeps())
                        acc.add_r(tk)
                        y32.set_w(tk)
                        tk = P.op("dve", lambda v, o=y.ap, i=y32.ap, g=gains_sb[:, gcol:gcol + 1], r=rs.ap: v.scalar_tensor_tensor(out=o, in0=i, scalar=g, in1=r, op0=ALU.mult, op1=ALU.mult),
                                  y32.rdeps() + rs.rdeps() + y.wdeps() + [gains_tok])
                        y32.add_r(tk)
                        rs.add_r(tk)
                        y.set_w(tk)
                        src_tok_holder = y
                        src_ap = y.ap
                    DBG = int(os.environ.get("P1_DBG", "9"))
                    if DBG == 1:
                        tk = P.op("pool", lambda g, o=ob.ap, a_=y.ap: g.tensor_copy(out=o, in_=a_), y.rdeps() + ob.wdeps())
                        y.add_r(tk)
                        ob.set_w(tk)
                        dtk = P.dma("sp", lambda s, o=qkT[fc, :, ts], i=ob.ap: s.dma_start(out=o, in_=i), ob.rdeps(), key=ob)
                        ob.add_r(dtk)
                        out_toks.append(dtk)
                        continue
                    aux = aux_ring.next()
                    sw = swB if isB else swA
                    tk = P.op("pe", lambda pe, o=aux.ap, l=sw, r=y.ap: pe.matmul(o, lhsT=l, rhs=r, start=True, stop=True), y.rdeps() + aux.wdeps() + [cst_tok])
                    y.add_r(tk)
                    aux.set_w(tk)
                    if DBG == 2:
                        tk = P.op("dve", lambda v, o=ob.ap, a_=aux.ap: v.tensor_copy(out=o, in_=a_), aux.rdeps() + ob.wdeps())
                        aux.add_r(tk)
                        ob.set_w(tk)
                        dtk = P.dma("sp", lambda s, o=qkT[fc, :, ts], i=ob.ap: s.dma_start(out=o, in_=i), ob.rdeps(), key=ob)
                        ob.add_r(dtk)
                        out_toks.append(dtk)
                        continue
                    t1 = t1_ring.next()
                    tk = P.op("dve", lambda v, o=t1.ap, i=src_ap, c=rt.ap[:, 0, :]: v.tensor_tensor(out=o, in0=i, in1=c, op=ALU.mult),
                              src_tok_holder.rdeps() + rt.rdeps() + t1.wdeps())
                    src_tok_holder.add_r(tk)
                    rt.add_r(tk)
                    t1.set_w(tk)
                    if DBG == 3:
                        tk = P.op("pool", lambda g, o=ob.ap, a_=t1.ap: g.tensor_copy(out=o, in_=a_), t1.rdeps() + ob.wdeps())
                        t1.add_r(tk)
                        ob.set_w(tk)
                        dtk = P.dma("sp", lambda s, o=qkT[fc, :, ts], i=ob.ap: s.dma_start(out=o, in_=i), ob.rdeps(), key=ob)
                        ob.add_r(dtk)
                        out_toks.append(dtk)
                        continue
                    a32 = a32_ring.next()
                    tk = P.op("act", lambda a, o=a32.ap, i=aux.ap: a.activation(out=o, in_=i, func=AF.Copy), aux.rdeps() + a32.wdeps())
                    aux.add_r(tk)
                    a32.set_w(tk)
                    t2 = t2_ring.next()
                    tk = P.op("dve", lambda v, o=t2.ap, i=a32.ap, c=rt.ap[:, 1, :]: v.tensor_tensor(out=o, in0=i, in1=c, op=ALU.mult),
                              a32.rdeps() + rt.rdeps() + t2.wdeps())
                    a32.add_r(tk)
                    rt.add_r(tk)
                    t2.set_w(tk)
                    tk = P.op(os.environ.get("P1_ADD", "dve"), lambda g, o=ob.ap, a_=t1.ap, b_=t2.ap: g.tensor_tensor(out=o, in0=a_, in1=b_, op=ALU.add), t1.rdeps() + t2.rdeps() + ob.wdeps())
                    t1.add_r(tk)
                    t2.add_r(tk)
                    ob.set_w(tk)
                dtk = P.dma("sp", lambda s, o=qkT[fc, :, ts], i=ob.ap: s.dma_start(out=o, in_=i), ob.rdeps(), key=ob)
                ob.add_r(dtk)
                out_toks.append(dtk)

        vacc_ring = C.ps_ring(1, [128, 640], F32)
        vs_ring = C.sb_ring(2, [128, 640], BF16)
        for t in range(NT if os.environ.get('P1_V', '1') == '1' else 0):
            va = vacc_ring.next()
            last = None
            for (c0, c1) in ((0, 512), (512, 640)):
                for kc in range(8):
                    last = P.op("pe", lambda pe, o=va.ap[:, c0:c1], l=xT[:, kc, t * 128:(t + 1) * 128], r=wv_sb[:, kc, c0:c1], k=kc: pe.matmul(o, lhsT=l, rhs=r, start=(k == 0), stop=(k == 7)),
                                xT_buf.rdeps() + [wv_tok] + va.wdeps())
            va.set_w(last)
            vs = vs_ring.next()
            tk = P.op("act", lambda a, o=vs.ap, i=va.ap: a.activation(out=o, in_=i, func=AF.Copy), va.rdeps() + vs.wdeps())
            va.add_r(tk)
            vs.set_w(tk)
            dtk = P.dma("sp", lambda s, o=vtok[t * 128:(t + 1) * 128, :], i=vs.ap: s.dma_start(out=o, in_=i), vs.rdeps(), key=vs)
            vs.add_r(dtk)
            out_toks.append(dtk)
        P.wait_all("sp", out_toks)
        P.emit()
    return nc


def _rot_tables(c0):
    pos = np.arange(c0, c0 + T)
    f32 = np.float32
    tabs = np.zeros((4, 64, T), f32)
    tabs[0] = 1.0
    invA = (f32(500000.0) ** (-(np.arange(8, dtype=f32)) / f32(8))).astype(f32)
    angA = (pos.astype(f32)[:, None] * invA[None, :]).astype(f32)
    tabs[0, 0:8] = np.cos(angA).T
    tabs[0, 8:16] = np.cos(angA).T
    tabs[1, 0:8] = -np.sin(angA).T
    tabs[1, 8:16] = np.sin(angA).T
    invB = (f32(10000.0) ** (-(np.arange(16, dtype=f32)) / f32(16))).astype(f32)
    row = (pos // 64).astype(f32)
    col = (pos % 64).astype(f32)
    angR = (row[:, None] * invB[None, :]).astype(f32)
    angC = (col[:, None] * invB[None, :]).astype(f32)
    tabs[2, 0:16] = np.cos(angR).T
    tabs[2, 16:32] = np.cos(angR).T
    tabs[2, 32:48] = np.cos(angC).T
    tabs[2, 48:64] = np.cos(angC).T
    tabs[3, 0:16] = -np.sin(angR).T
    tabs[3, 16:32] = np.sin(angR).T
    tabs[3, 32:48] = -np.sin(angC).T
    tabs[3, 48:64] = np.sin(angC).T
    return np.ascontiguousarray(np.concatenate([tabs, tabs], axis=1))


def _p1_consts():
    c = np.zeros((4, 128, 128), np.float32)
    c[0] = np.eye(128, dtype=np.float32)
    for h in range(2):
        b = h * 64
        for d in range(16):
            c[1, b + (d + 8) % 16, b + d] = 1.0
        for d in range(64):
            g = (d // 32) * 32
            c[2, b + g + ((d - g) + 16) % 32, b + d] = 1.0
        c[3, b:b + 64, b:b + 64] = 1.0 / 64.0
    return c


def _qk_cols():
    cols = []
    cols += list(range(0, 256))
    cols += list(range(256, 512))
    for i in range(4):
        cols += list(range(768 + 64 * i, 768 + 64 * i + 64))
        cols += list(range(768 + 64 * (i + 4), 768 + 64 * (i + 4) + 64))
    cols += list(range(1280, 1408))
    cols += list(range(1536, 1792))
    cols += list(range(1792, 2048))
    return np.array(cols)


def _v_cols():
    return np.array(list(range(512, 768)) + list(range(1408, 1536)) + list(range(2048, 2304)))


def _a_tiles():
    tl = []
    for d, ntile, nq in ((1, 17, 2048), (4, 5, 512), (16, 2, 128)):
        for r in range(d):
            for i in range(ntile):
                e0 = 1024 + d * (128 * i - 64) + r
                u_lo = max(0, 128 * i - 128)
                u_hi = min(nq, 128 * i + 128)
                tl.append((d, r, e0, u_lo, u_hi - u_lo, u_lo - (128 * i - 128)))
    return tl


def build_p2():
    nc = bass.Bass("TRN2", target_bir_lowering=False)
    qk = dram_in(nc, "qk", [13, 128, T], BF16)
    kaT = dram_in(nc, "kaT", [2, 128, 4096], BF16)
    va = dram_in(nc, "va", [4096, 2, 192], BF16)
    kbT = dram_in(nc, "kbT", [128, S], BF16)
    vb = dram_in(nc, "vb", [S, 192], BF16)
    kcT = dram_in(nc, "kcT", [2, 128, 2560], BF16)
    vc = dram_in(nc, "vc", [2560, 2, 192], BF16)
    bias2 = dram_in(nc, "bias2", [4, 128, 1408])
    cmask = dram_in(nc, "cmask", [3, 128, 1408])
    band = dram_in(nc, "band", [128, 256])
    flags = dram_in(nc, "flags", [128, 2])
    oT = dram_out(nc, "oT", [8, 128, T], BF16)
    P = Prog(nc)
    with ExitStack() as st:
        C = Ctx(nc, st)
        psum = C.ps([128, 8, 512], F32, "psall")
        ka_sb = C.sb([128, 2, 4096], BF16, "ka")
        va_sb = C.sb([128, 69, 192], BF16, "va")
        kb_sb = C.sb([128, S], BF16, "kb")
        vb_sb = C.sb([128, 64, 192], BF16, "vb")
        kc_sb = C.sb([128, 2, 2560], BF16, "kc")
        vc_sb = C.sb([128, 2, 20, 192], BF16, "vc")
        band_sb = C.sb([128, 256], BF16, "band")
        cm_sb = C.sb([128, 3, 1408], BF16, "cm")
        fl_sb = C.sb([128, 2], F32, "fl")
        zero_sb = C.sb([128, 128], BF16, "zero")
        bias_sb = C.sb([128, 1408], F32, "bias")
        eall_sb = C.sb([128, 1408], BF16, "eall")
        eint_sb = C.sb([128, 1408], BF16, "eint")
        etmp_sb = C.sb([128, 4, 256], BF16, "etmp")
        espc_sb = C.sb([128, 4, 256], BF16, "espc")
        rc_sb = C.sb([128, T], F32, "rc")
        q_ring = C.sb_ring(2, [128, T], BF16)
        o_ring = C.sb_ring(2, [128, T], BF16)
        PT_ring = C.sb_ring(3, [128, 1024], BF16)
        PM_ring = C.sb_ring(3, [128, 512], BF16)
        out_toks = []

        ka_tok = P.dma("sp", lambda s: s.dma_start(out=ka_sb[:], in_=kaT.rearrange("c p f -> p c f")))
        kb_tok = P.dma("sp", lambda s: s.dma_start(out=kb_sb[:], in_=kbT[:, :]))
        vb_tok = P.dma("sp", lambda s: s.dma_start(out=vb_sb[:], in_=vb.rearrange("(t p) f -> p t f", p=128)))
        kc_tok = P.dma("sp", lambda s: s.dma_start(out=kc_sb[:], in_=kcT.rearrange("c p f -> p c f")))
        vc_tok = P.dma("sp", lambda s: s.dma_start(out=vc_sb[:], in_=vc.rearrange("(t p) c f -> p c t f", p=128)))
        band_tok = P.dma("pool", lambda g: g.dma_start(out=band_sb[:], in_=band[:, :]))
        cm_tok = P.dma("pool", lambda g: g.dma_start(out=cm_sb[:], in_=cmask.rearrange("c p f -> p c f")))
        fl_tok = P.dma("sp", lambda s: s.dma_start(out=fl_sb[:], in_=flags[:, :]))
        zero_tok = P.op("pool", lambda g: g.memset(zero_sb[:], 0.0))
        rc_buf = Buf(rc_sb[:])

        def load_q(chunk):
            q = q_ring.next()
            tk = P.dma("sp", lambda s, o=q.ap, i=qk[chunk, :, :]: s.dma_start(out=o, in_=i), q.wdeps(), key=q)
            q.set_w(tk)
            return q

        nb_sb = C.sb([128, T], F32, "nb")
        nb_buf = Buf(nb_sb[:])

        def normalize(acc_full, half, ob, cols, accbuf, n):
            osl = slice(0, 64) if half == 0 else slice(64, 128)
            dsl = slice(64, 128) if half == 0 else slice(0, 64)
            tcp = P.op("dve", lambda v, o=nb_sb[:, 0:n], i=acc_full: v.tensor_copy(out=o, in_=i), accbuf.rdeps() + nb_buf.wdeps())
            accbuf.add_r(tcp)
            nb_buf.set_w(tcp)
            t1 = P.op("dve", lambda v, o=rc_sb[osl, 0:n], i=nb_sb[dsl, 0:n]: v.reciprocal(out=o, in_=i), [tcp] + rc_buf.wdeps())
            rc_buf.set_w(t1)
            t2 = P.op("dve", lambda v, o=ob.ap[osl, cols], a_=nb_sb[osl, 0:n], b_=rc_sb[osl, 0:n]: v.tensor_tensor(out=o, in0=a_, in1=b_, op=ALU.mult), ob.wdeps() + [t1, tcp])
            rc_buf.add_r(t2)
            nb_buf.add_r(t2)
            return t2

        S_ring = Ring([psum[:, 0:2, :], psum[:, 2:4, :]])
        acc_ring = Ring([psum[:, 4:6, :], psum[:, 6:8, :]])
        phase_toks = []
        for i in range(4):
            q = load_q(4 + i)
            ob = o_ring.next()
            wt = []
            for qb in range(4):
                qs = slice(qb * 512, (qb + 1) * 512)
                acc = acc_ring.next()
                tk = None
                for kt in range(64):
                    ks = slice(kt * 128, (kt + 1) * 128)
                    Sb = S_ring.next()
                    P.op("pe", lambda pe, o=Sb.ap[:, 0, :], l=kb_sb[0:64, ks], r=q.ap[0:64, qs]: pe.matmul(o, lhsT=l, rhs=r, start=True, stop=True),
                         [kb_tok] + q.rdeps() + Sb.wdeps())
                    tk = P.op("pe", lambda pe, o=Sb.ap[:, 1, :], l=kb_sb[64:128, ks], r=q.ap[64:128, qs]: pe.matmul(o, lhsT=l, rhs=r, start=True, stop=True))
                    Sb.set_w(tk)
                    PT = PT_ring.next()
                    tk = P.op("act", lambda a, o=PT.ap, i_=Sb.ap.rearrange("p a b -> p (a b)"): a.activation(out=o, in_=i_, func=AF.Exp, scale=SCALE), Sb.rdeps() + PT.wdeps())
                    Sb.add_r(tk)
                    PT.set_w(tk)
                    P.op("pe", lambda pe, o=acc.ap[:, 0, :], l=vb_sb[:, kt, 0:128], r=PT.ap[:, 0:512], k=kt: pe.matmul(o, lhsT=l, rhs=r, start=(k == 0), stop=(k == 63)),
                         PT.rdeps() + [vb_tok] + (acc.wdeps() if kt == 0 else []))
                    tk = P.op("pe", lambda pe, o=acc.ap[:, 1, :], l=vb_sb[:, kt, 64:192], r=PT.ap[:, 512:1024], k=kt: pe.matmul(o, lhsT=l, rhs=r, start=(k == 0), stop=(k == 63)))
                    PT.add_r(tk)
                acc.set_w(tk)
                q.add_r(tk)
                normalize(acc.ap[:, 0, :], 0, ob, qs, acc, 512)
                t4 = normalize(acc.ap[:, 1, :], 1, ob, qs, acc, 512)
                wt.append(t4)
            ob.set_w(*wt)
            dtk = P.dma("sp", lambda s, o=oT[2 + i, :, :], i_=ob.ap: s.dma_start(out=o, in_=i_), ob.rdeps(), key=ob)
            ob.add_r(dtk)
            out_toks.append(dtk)
            phase_toks = [wt[-1], tk]

        accA = psum[:, 0:4, :].rearrange("p a b -> p (a b)")
        accA_buf = Buf(accA)
        accA_buf.r = list(phase_toks)
        SA_ring = Ring([psum[:, 4 + k, 0:256] for k in range(4)])
        for b_ in SA_ring.bufs:
            b_.r = list(phase_toks)
        tiles = _a_tiles()
        va_buf = Buf(va_sb[:])
        for p in range(2):
            q = load_q(p)
            ob = o_ring.next()
            vtoks = []
            for ti, (d, r, e0, u_lo, n, qp0) in enumerate(tiles):
                vtk = P.dma("sp", lambda s, o=va_sb[:, ti, :], i_=va[e0:e0 + 128 * d - (d - 1):d, p, :]: s.dma_start(out=o, in_=i_), va_buf.wdeps(), key="va")
                vtoks.append(vtk)
            va_buf.set_w(*vtoks)
            wt = []
            for half in range(2):
                base = 64 * half
                for z4 in range(4):
                    tk = P.op("pe", lambda pe, o=accA[:, z4 * 512:(z4 + 1) * 512], r_=ka_sb[:, 0, 0:512]: pe.matmul(o, lhsT=zero_sb[:], rhs=r_, start=True, stop=False),
                              [zero_tok, ka_tok] + accA_buf.wdeps())
                tk = None
                for ti, (d, r, e0, u_lo, n, qp0) in enumerate(tiles):
                    q0 = d * u_lo + r
                    qsl = slice(q0, q0 + d * (n - 1) + 1, d)
                    ksl = slice(e0, e0 + d * 127 + 1, d)
                    Sb = SA_ring.next()
                    tk = P.op("pe", lambda pe, o=Sb.ap[:, 0:n], l=ka_sb[base:base + 64, p, ksl], r_=q.ap[base:base + 64, qsl]: pe.matmul(o, lhsT=l, rhs=r_, start=True, stop=True),
                              [ka_tok] + q.rdeps() + Sb.wdeps())
                    Sb.set_w(tk)
                    PT = PT_ring.next()
                    tk = P.op("act", lambda a, o=PT.ap[:, 0:n], i_=Sb.ap[:, 0:n]: a.activation(out=o, in_=i_, func=AF.Exp, scale=SCALE), Sb.rdeps() + PT.wdeps())
                    Sb.add_r(tk)
                    PT.set_w(tk)
                    PM = PM_ring.next()
                    tk = P.op("dve", lambda g, o=PM.ap[:, 0:n], a_=PT.ap[:, 0:n], m=band_sb[:, qp0:qp0 + n]: g.tensor_tensor(out=o, in0=a_, in1=m, op=ALU.mult),
                              PT.rdeps() + PM.wdeps() + [band_tok])
                    PT.add_r(tk)
                    PM.set_w(tk)
                    lv = va_sb[:, ti, 0:128] if half == 0 else va_sb[:, ti, 64:192]
                    last = ti == len(tiles) - 1
                    ustart = u_lo
                    first_piece = True
                    while ustart < u_lo + n:
                        bank = (d * ustart + r) // 512
                        uend = min(u_lo + n, -(-(512 * (bank + 1) - r) // d))
                        m = uend - ustart
                        c0_ = d * ustart + r
                        osl = slice(c0_, c0_ + d * (m - 1) + 1, d)
                        off = ustart - u_lo
                        tk = P.op("pe", lambda pe, o=accA[:, osl], l=lv, r_=PM.ap[:, off:off + m], la=(last and uend == u_lo + n): pe.matmul(o, lhsT=l, rhs=r_, start=False, stop=False),
                                  (PM.rdeps() + va_buf.rdeps()) if first_piece else [])
                        first_piece = False
                        ustart = uend
                    PM.add_r(tk)
                for z4 in range(4):
                    tk = P.op("pe", lambda pe, o=accA[:, z4 * 512:(z4 + 1) * 512], r_=ka_sb[:, 0, 0:512]: pe.matmul(o, lhsT=zero_sb[:], rhs=r_, start=False, stop=True))
                accA_buf.set_w(tk)
                va_buf.add_r(tk)
                q.add_r(tk)
                t4 = normalize(accA, half, ob, slice(0, T), accA_buf, T)
                wt.append(t4)
            ob.set_w(*wt)
            dtk = P.dma("sp", lambda s, o=oT[p, :, :], i_=ob.ap: s.dma_start(out=o, in_=i_), ob.rdeps(), key=ob)
            ob.add_r(dtk)
            out_toks.append(dtk)
            phase_toks = [wt[-1], tk]

        SC_ring = Ring([psum[:, k, :] for k in range(4)])
        accC_ring = Ring([psum[:, 4 + k, :] for k in range(4)])
        for b_ in SC_ring.bufs + accC_ring.bufs:
            b_.r = list(phase_toks)
        e_buf = Buf()
        spec_slices = [(0, 4, 6, 1), (0, 5, 4, 1), (3, 2, 14, 2), (3, 3, 12, 2)]
        for p in range(2):
            q = load_q(9 + p)
            ob = o_ring.next()
            wt = []
            for half in range(2):
                h = 2 * p + half
                base = 64 * half
                btk = P.dma("sp", lambda s, i_=bias2[h, :, :]: s.dma_start(out=bias_sb[:], in_=i_), e_buf.wdeps(), key="bias")
                tk = P.op("act", lambda a: a.activation(out=eall_sb[:], in_=bias_sb[:], func=AF.Exp), [btk] + e_buf.wdeps())
                tk1 = P.op("dve", lambda g: g.tensor_tensor(out=eint_sb[:], in0=eall_sb[:], in1=cm_sb[:, 0, :], op=ALU.mult), [tk, cm_tok] + e_buf.wdeps())
                etoks = [tk1]
                for si, (sb_, skt, blk0, mi) in enumerate(spec_slices):
                    cs = slice(blk0 * 64, blk0 * 64 + 256)
                    tk2 = P.op("dve", lambda g, o=etmp_sb[:, si, :], a_=eall_sb[:, cs], m=cm_sb[:, mi, cs]: g.tensor_tensor(out=o, in0=a_, in1=m, op=ALU.mult), [tk])
                    tk3 = P.op("dve", lambda v, o=espc_sb[:, si, :], a_=etmp_sb[:, si, :], f=fl_sb[:, mi - 1:mi], b2=eint_sb[:, cs]: v.scalar_tensor_tensor(out=o, in0=a_, scalar=f, in1=b2, op0=ALU.mult, op1=ALU.add),
                               [tk2, tk1, fl_tok] + e_buf.wdeps())
                    etoks.append(tk3)
                e_buf.set_w(*etoks)
                for b in range(4):
                    qs = slice(b * 512, (b + 1) * 512)
                    acc = accC_ring.next()
                    tk = None
                    for kt in range(8):
                        er = 8 * b + 2 * kt
                        Sb = SC_ring.next()
                        tk = P.op("pe", lambda pe, o=Sb.ap, l=kc_sb[base:base + 64, p, er * 64:er * 64 + 128], r_=q.ap[base:base + 64, qs]: pe.matmul(o, lhsT=l, rhs=r_, start=True, stop=True),
                                  [kc_tok] + q.rdeps() + Sb.wdeps())
                        Sb.set_w(tk)
                        PT = PT_ring.next()
                        tk = P.op("act", lambda a, o=PT.ap[:, 0:512], i_=Sb.ap: a.activation(out=o, in_=i_, func=AF.Exp, scale=SCALE), Sb.rdeps() + PT.wdeps())
                        Sb.add_r(tk)
                        PT.set_w(tk)
                        PM = PM_ring.next()
                        blk0 = 14 - 2 * kt
                        segs = [(0, 512, eint_sb[:, blk0 * 64:blk0 * 64 + 512])]
                        for si, (sb_, skt, sblk0, mi) in enumerate(spec_slices):
                            if sb_ == b and skt == kt:
                                if mi == 1:
                                    segs = [(0, 256, espc_sb[:, si, :]), (256, 512, eint_sb[:, blk0 * 64 + 256:blk0 * 64 + 512])]
                                else:
                                    segs = [(0, 256, eint_sb[:, blk0 * 64:blk0 * 64 + 256]), (256, 512, espc_sb[:, si, :])]
                        for (c0, c1, m) in segs:
                            tk = P.op("dve", lambda g, o=PM.ap[:, c0:c1], a_=PT.ap[:, c0:c1], m_=m: g.tensor_tensor(out=o, in0=a_, in1=m_, op=ALU.mult),
                                      PT.rdeps() + PM.wdeps() + e_buf.rdeps())
                        PT.add_r(tk)
                        PM.set_w(tk)
                        e_buf.add_r(tk)
                        lv = vc_sb[:, p, 4 * b + kt, 0:128] if half == 0 else vc_sb[:, p, 4 * b + kt, 64:192]
                        tk = P.op("pe", lambda pe, o=acc.ap, l=lv, r_=PM.ap, k=kt: pe.matmul(o, lhsT=l, rhs=r_, start=(k == 0), stop=(k == 7)),
                                  PM.rdeps() + [vc_tok] + (acc.wdeps() if kt == 0 else []))
                        PM.add_r(tk)
                    acc.set_w(tk)
                    q.add_r(tk)
                    t4 = normalize(acc.ap, half, ob, qs, acc, 512)
                    wt.append(t4)
            ob.set_w(*wt)
            dtk = P.dma("sp", lambda s, o=oT[6 + p, :, :], i_=ob.ap: s.dma_start(out=o, in_=i_), ob.rdeps(), key=ob)
            ob.add_r(dtk)
            out_toks.append(dtk)
        P.wait_all("sp", out_toks)
        P.emit()
    return nc


def _c_tables():
    cq = np.arange(64)
    c0 = np.clip(cq - 8, 0, 48)
    ck = np.arange(64)
    colmask = (ck[:, None] >= c0[None, :]) & (ck[:, None] < c0[None, :] + 16)
    m = np.zeros((3, 128, 22 * 64), np.float32)
    for u in range(2):
        for blk in range(22):
            dr = 7 + u - (blk - 3)
            sl = (slice(u * 64, u * 64 + 64), slice(blk * 64, blk * 64 + 64))
            if -4 <= dr <= 3:
                m[0][sl] = colmask
            if 4 <= dr <= 7:
                m[1][sl] = colmask
            if -7 <= dr <= -5:
                m[2][sl] = colmask
    return m


def _c_bias(rpb):
    out = np.zeros((4, 128, 22 * 64), np.float32)
    ck = np.arange(64)[:, None]
    cq = np.arange(64)[None, :]
    dc = np.clip(ck - cq + 15, 0, 30)
    for u in range(2):
        for blk in range(22):
            dr = 7 + u - (blk - 3)
            if -7 <= dr <= 7:
                out[:, u * 64:u * 64 + 64, blk * 64:blk * 64 + 64] = rpb[:, dr + 7][:, dc]
    return out


def _band():
    k = np.arange(128)[:, None]
    q = np.arange(256)[None, :]
    return ((q >= k) & (q <= k + 128)).astype(np.float32)


def _ext(arr, axis, lo, n):
    size = arr.shape[axis]
    shp = list(arr.shape)
    shp[axis] = n
    out = np.zeros(shp, arr.dtype)
    a = max(lo, 0)
    b = min(lo + n, size)
    src = [slice(None)] * arr.ndim
    dst = [slice(None)] * arr.ndim
    src[axis] = slice(a, b)
    dst[axis] = slice(a - lo, b - lo)
    out[tuple(dst)] = arr[tuple(src)]
    return out


def _gather_p2(qk_list, v_list, rpb):
    cm = _c_tables()
    bias2 = _c_bias(rpb)
    band = _band()
    maps = []
    one = np.ones((S, 64), BF)
    for b in range(2):
        cores = [4 * b + j for j in range(4)]
        kaT_b = np.concatenate([qk_list[c][2:4] for c in cores], axis=2)
        kbT_b = np.concatenate([qk_list[c][8] for c in cores], axis=1)
        kcT_b = np.concatenate([qk_list[c][11:13] for c in cores], axis=2)
        v_b = np.concatenate([v_list[c] for c in cores], axis=0)
        va_b = np.stack([np.concatenate([v_b[:, 128 * p:128 * p + 64], one, v_b[:, 128 * p + 64:128 * p + 128]], axis=1) for p in range(2)], axis=1)
        vb_b = np.ascontiguousarray(np.concatenate([v_b[:, 256:320], one, v_b[:, 320:384]], axis=1))
        vc_b = np.stack([np.concatenate([v_b[:, 384 + 128 * p:384 + 128 * p + 64], one, v_b[:, 384 + 128 * p + 64:384 + 128 * p + 128]], axis=1) for p in range(2)], axis=1)
        for j in range(4):
            c0 = j * T
            fl = np.zeros((128, 2), np.float32)
            fl[:, 0] = 1.0 if j == 0 else 0.0
            fl[:, 1] = 1.0 if j == 3 else 0.0
            maps.append({
                "qk": qk_list[4 * b + j],
                "kaT": _ext(kaT_b, 2, c0 - 1024, 4096),
                "va": _ext(va_b, 0, c0 - 1024, 4096),
                "kbT": kbT_b, "vb": vb_b,
                "kcT": _ext(kcT_b, 2, c0 - 256, 2560),
                "vc": _ext(vc_b, 0, c0 - 256, 2560),
                "bias2": bias2, "cmask": cm, "band": band, "flags": fl,
            })
    return maps


class LNState:
    def __init__(self, P, C, ln_dram):
        self.P = P
        self.tab = C.sb([128, 2, D], F32, "lntab")
        self.tab_tok = P.dma("sp", lambda s: s.dma_start(out=self.tab[:], in_=ln_dram.rearrange("c p f -> p c f")))
        self.eps = C.sb([128, 1], F32, "lneps")
        self.eps_tok = P.op("pool", lambda g: g.memset(self.eps[:], LN_EPS))
        self.st_ring = C.sb_ring(2, [128, 2, 6], F32)
        self.mv_ring = C.sb_ring(2, [128, 4], F32)
        self.o_ring = C.sb_ring(2, [128, D], F32)

    def emit(self, y_ap, y_deps, dst_dram):
        P = self.P
        st = self.st_ring.next()
        mv = self.mv_ring.next()
        a = P.op("dve", lambda v, o=st.ap[:, 0, :], i=y_ap[:, 0:512]: v.bn_stats(out=o, in_=i), y_deps + st.wdeps())
        b = P.op("dve", lambda v, o=st.ap[:, 1, :], i=y_ap[:, 512:1024]: v.bn_stats(out=o, in_=i), [a])
        st.set_w(a, b)
        c = P.op("dve", lambda v, o=mv.ap[:, 0:2], i=st.ap.rearrange("p a b -> p (a b)"): v.bn_aggr(out=o, in_=i), [a, b] + mv.wdeps())
        st.add_r(c)
        d = P.op("act", lambda a_, o=mv.ap[:, 2:3], i=mv.ap[:, 1:2]: a_.activation(out=o, in_=i, func=AF.Sqrt, bias=self.eps[:, 0:1], scale=1.0), [c, self.eps_tok])
        e = P.op("dve", lambda v, o=mv.ap[:, 2:3]: v.reciprocal(out=o, in_=o), [d])
        f = P.op("dve", lambda v, o=mv.ap[:, 3:4], m=mv.ap[:, 0:1], r=mv.ap[:, 2:3]: v.scalar_tensor_tensor(out=o, in0=m, scalar=-1.0, in1=r, op0=ALU.mult, op1=ALU.mult), [e])
        mv.set_w(f)
        ob = self.o_ring.next()
        g = P.op("act", lambda a_, o=ob.ap, i=y_ap, sc=mv.ap[:, 2:3], bi=mv.ap[:, 3:4]: a_.activation(out=o, in_=i, func=AF.Identity, bias=bi, scale=sc), [f] + y_deps + ob.wdeps())
        mv.add_r(g)
        h = P.op("dve", lambda v, o=ob.ap, t=self.tab[:, 0, :]: v.tensor_tensor(out=o, in0=o, in1=t, op=ALU.mult), [g, self.tab_tok])
        i2 = P.op("dve", lambda v, o=ob.ap, t=self.tab[:, 1, :]: v.tensor_tensor(out=o, in0=o, in1=t, op=ALU.add), [h])
        ob.set_w(i2)
        dtk = P.dma("sp", lambda s, o=dst_dram, i=ob.ap: s.dma_start(out=o, in_=i), [i2], key=ob)
        ob.add_r(dtk)
        return dtk, g


BR = [(0, 2, 0), (256, 4, 2), (768, 2, 6)]


def build_p3a():
    nc = bass.Bass("TRN2", target_bir_lowering=False)
    x = dram_in(nc, "x", [T, D])
    oT = dram_in(nc, "oT", [8, 128, T], BF16)
    wg = dram_in(nc, "wg", [D, 3 * D])
    bg = dram_in(nc, "bg", [128, 24])
    wbr = dram_in(nc, "wbr", [D, D])
    wout = dram_in(nc, "wout", [D, D])
    ln = dram_in(nc, "ln", [2, 128, D])
    cst = dram_in(nc, "cst", [128, 128])
    x1 = dram_out(nc, "x1", [T, D], F32)
    P = Prog(nc)
    with ExitStack() as st:
        C = Ctx(nc, st)
        xT = C.sb([128, 8, T], BF16, "xT")[:]
        xT_buf = Buf()
        o_sb = C.sb([128, 8, T], BF16, "osb")
        mT = C.sb([128, 8, T], BF16, "mT")
        mT_buf = Buf()
        ident = C.sb([128, 128], BF16, "ident")
        bg_sb = C.sb([128, 24], F32, "bg")
        wout_sb = C.sb([128, 8, D], BF16, "wout")
        id_tok = P.dma("pool", lambda g: g.dma_start(out=ident[:], in_=cst[:, :]))
        bg_tok = P.dma("sp", lambda s: s.dma_start(out=bg_sb[:], in_=bg[:, :]))
        o_tok = P.dma("sp", lambda s: s.dma_start(out=o_sb[:], in_=oT.rearrange("c p f -> p c f")))
        wout_tok = P.dma("pool", lambda g: g.dma_start(out=wout_sb[:], in_=wout.rearrange("(kc p) f -> p kc f", p=128)))
        L = LNState(P, C, ln)
        xld = C.sb_ring(2, [128, D], F32)

        def get_tile(t):
            b = xld.next()
            tk = P.dma("sp", lambda s, o=b.ap, i=x[t * 128:(t + 1) * 128, :]: s.dma_start(out=o, in_=i), b.wdeps(), key=b)
            b.set_w(tk)
            return b.ap, [tk], b.add_r

        emit_make_xT(P, C, xT, xT_buf, ident[:], id_tok, get_tile)

        wg_ring = C.sb_ring(3, [128, 8, 128], BF16)
        wb_ring = C.sb_ring(3, [128, 4, 128], BF16)
        gl_ring = C.ps_ring(2, [128, 512], F32)
        pj_ring = C.ps_ring(2, [128, 512], F32)
        g_ring = C.sb_ring(2, [128, 512], F32)
        pjs_ring = C.sb_ring(2, [128, 512], F32)
        tmp_ring = C.sb_ring(2, [128, 512], F32)
        macc = [Buf(C.sb([128, 512], F32)[:]) for _ in range(4)]
        mtoks = []
        for fc in range(8):
            for br in range(3):
                roff, nk, ch0 = BR[br]
                wgs = wg_ring.next()
                tk = P.dma("pool", lambda g, o=wgs.ap, i=wg[:, br * D + fc * 128:br * D + (fc + 1) * 128].rearrange("(kc p) f -> p kc f", p=128): g.dma_start(out=o, in_=i), wgs.wdeps(), key=wgs)
                wgs.set_w(tk)
                wbs = wb_ring.next()
                tk = P.dma("pool", lambda g, o=wbs.ap[:, 0:nk, :], i=wbr[roff:roff + nk * 128, fc * 128:(fc + 1) * 128].rearrange("(kc p) f -> p kc f", p=128): g.dma_start(out=o, in_=i), wbs.wdeps(), key=wbs)
                wbs.set_w(tk)
                for tb in range(4):
                    ts = slice(tb * 512, (tb + 1) * 512)
                    gl = gl_ring.next()
                    last = None
                    for kc in range(8):
                        last = P.op("pe", lambda pe, o=gl.ap, l=wgs.ap[:, kc, :], r=xT[:, kc, ts], k=kc: pe.matmul(o, lhsT=l, rhs=r, start=(k == 0), stop=(k == 7)),
                                    wgs.rdeps() + xT_buf.rdeps() + gl.wdeps())
                    wgs.add_r(last)
                    gl.set_w(last)
                    pj = pj_ring.next()
                    for kc in range(nk):
                        last = P.op("pe", lambda pe, o=pj.ap, l=wbs.ap[:, kc, :], r=o_sb[:, ch0 + kc, ts], k=kc, n_=nk: pe.matmul(o, lhsT=l, rhs=r, start=(k == 0), stop=(k == n_ - 1)),
                                    wbs.rdeps() + [o_tok] + pj.wdeps())
                    wbs.add_r(last)
                    pj.set_w(last)
                    g_ = g_ring.next()
                    tk = P.op("act", lambda a, o=g_.ap, i=gl.ap, b_=bg_sb[:, br * 8 + fc:br * 8 + fc + 1]: a.activation(out=o, in_=i, func=AF.Sigmoid, bias=b_, scale=1.0), gl.rdeps() + g_.wdeps() + [bg_tok])
                    gl.add_r(tk)
                    g_.set_w(tk)
                    pjs = pjs_ring.next()
                    tk = P.op("act", lambda a, o=pjs.ap, i=pj.ap: a.activation(out=o, in_=i, func=AF.Copy), pj.rdeps() + pjs.wdeps())
                    pj.add_r(tk)
                    pjs.set_w(tk)
                    mc = macc[tb]
                    if br == 0:
                        tk = P.op("dve", lambda v, o=mc.ap, a_=g_.ap, b_=pjs.ap: v.tensor_tensor(out=o, in0=a_, in1=b_, op=ALU.mult), g_.rdeps() + pjs.rdeps() + mc.wdeps())
                        g_.add_r(tk)
                        pjs.add_r(tk)
                        mc.set_w(tk)
                    else:
                        tmp = tmp_ring.next()
                        tk = P.op("dve", lambda v, o=tmp.ap, a_=g_.ap, b_=pjs.ap: v.tensor_tensor(out=o, in0=a_, in1=b_, op=ALU.mult), g_.rdeps() + pjs.rdeps() + tmp.wdeps())
                        g_.add_r(tk)
                        pjs.add_r(tk)
                        tmp.set_w(tk)
                        dst = mc.ap if br == 1 else mT[:, fc, ts]
                        tk2 = P.op("dve", lambda v, o=dst, a_=mc.ap, b_=tmp.ap: v.tensor_tensor(out=o, in0=a_, in1=b_, op=ALU.add), [tk] + mc.rdeps() + mc.wdeps())
                        tmp.add_r(tk2)
                        if br == 1:
                            mc.set_w(tk2)
                        else:
                            mc.add_r(tk2)
                            mtoks.append(tk2)
        mT_buf.set_w(*mtoks)

        xr_ring = C.sb_ring(2, [128, D], F32)
        op_ring = C.ps_ring(2, [128, 512], F32)
        mix_ring = C.sb_ring(2, [128, D], F32)
        out_toks = []
        for t in range(NT):
            xr = xr_ring.next()
            tk = P.dma("sp", lambda s, o=xr.ap, i=x[t * 128:(t + 1) * 128, :]: s.dma_start(out=o, in_=i), xr.wdeps(), key=xr)
            xr.set_w(tk)
            mix = mix_ring.next()
            ctoks = []
            for half in range(2):
                hs = slice(half * 512, (half + 1) * 512)
                ps_ = op_ring.next()
                last = None
                for kc in range(8):
                    last = P.op("pe", lambda pe, o=ps_.ap, l=mT[:, kc, t * 128:(t + 1) * 128], r=wout_sb[:, kc, hs], k=kc: pe.matmul(o, lhsT=l, rhs=r, start=(k == 0), stop=(k == 7)),
                                mT_buf.rdeps() + [wout_tok] + ps_.wdeps())
                ps_.set_w(last)
                tk = P.op("act", lambda a, o=mix.ap[:, hs], i=ps_.ap: a.activation(out=o, in_=i, func=AF.Copy), ps_.rdeps() + mix.wdeps())
                ps_.add_r(tk)
                ctoks.append(tk)
            mix.set_w(*ctoks)
            tk = P.op("dve", lambda v, o=mix.ap, a_=xr.ap, b_=mix.ap: v.scalar_tensor_tensor(out=o, in0=a_, scalar=ALPHA, in1=b_, op0=ALU.mult, op1=ALU.add), xr.rdeps() + mix.rdeps())
            xr.add_r(tk)
            mix.set_w(tk)
            dtk, lr = L.emit(mix.ap, [tk], x1[t * 128:(t + 1) * 128, :])
            mix.add_r(lr)
            out_toks.append(dtk)
        P.wait_all("sp", out_toks)
        P.emit()
    return nc


def build_p3b():
    nc = bass.Bass("TRN2", target_bir_lowering=False)
    x = dram_in(nc, "x", [T, D])
    wup = dram_in(nc, "wup", [D, 4 * D])
    wdn = dram_in(nc, "wdn", [4 * D, D])
    ln = dram_in(nc, "ln", [2, 128, D])
    cst = dram_in(nc, "cst", [128, 128])
    x2 = dram_out(nc, "x2", [T, D], F32)
    P = Prog(nc)
    with ExitStack() as st:
        C = Ctx(nc, st)
        xT = C.sb([128, 8, T], BF16, "xT")[:]
        xT_buf = Buf()
        x_sb = C.sb([128, NT, D], F32, "xres")
        ident = C.sb([128, 128], BF16, "ident")
        id_tok = P.dma("pool", lambda g: g.dma_start(out=ident[:], in_=cst[:, :]))
        L = LNState(P, C, ln)
        xbufs = [Buf(x_sb[:, t, :]) for t in range(NT)]
        for t in range(NT):
            tk = P.dma("sp", lambda s, o=x_sb[:, t, :], i=x[t * 128:(t + 1) * 128, :]: s.dma_start(out=o, in_=i), key=("xl", t))
            xbufs[t].set_w(tk)

        def get_tile(t):
            return xbufs[t].ap, xbufs[t].rdeps(), xbufs[t].add_r

        emit_make_xT(P, C, xT, xT_buf, ident[:], id_tok, get_tile)
        for t in range(NT):
            tk = P.op("dve", lambda v, o=xbufs[t].ap: v.tensor_scalar(out=o, in0=o, scalar1=ALPHA, scalar2=None, op0=ALU.mult), xbufs[t].wdeps())
            xbufs[t].set_w(tk)

        wu_ring = C.sb_ring(2, [128, 8, D], BF16)
        wd_ring = C.sb_ring(2, [128, 8, D], BF16)
        hT_ring = C.sb_ring(1, [128, 8, 512], BF16)
        up_ring = C.ps_ring(2, [128, 512], F32)
        dn_ring = C.ps_ring(2, [128, 512], F32)
        r_ring = C.sb_ring(2, [128, 512], F32)
        ff_ring = C.sb_ring(2, [128, 512], F32)
        for qt in range(4):
            wu = wu_ring.next()
            tk = P.dma("pool", lambda g, o=wu.ap, i=wup[:, qt * D:(qt + 1) * D].rearrange("(kc p) f -> p kc f", p=128): g.dma_start(out=o, in_=i), wu.wdeps(), key=wu)
            wu.set_w(tk)
            wd = wd_ring.next()
            tk = P.dma("pool", lambda g, o=wd.ap, i=wdn[qt * D:(qt + 1) * D, :].rearrange("(kc p) f -> p kc f", p=128): g.dma_start(out=o, in_=i), wd.wdeps(), key=wd)
            wd.set_w(tk)
            for tb in range(4):
                ts = slice(tb * 512, (tb + 1) * 512)
                hT = hT_ring.next()
                htoks = []
                hwd = hT.wdeps()
                for fcq in range(8):
                    up = up_ring.next()
                    last = None
                    for kc in range(8):
                        last = P.op("pe", lambda pe, o=up.ap, l=wu.ap[:, kc, fcq * 128:(fcq + 1) * 128], r=xT[:, kc, ts], k=kc: pe.matmul(o, lhsT=l, rhs=r, start=(k == 0), stop=(k == 7)),
                                    wu.rdeps() + xT_buf.rdeps() + up.wdeps())
                    wu.add_r(last)
                    up.set_w(last)
                    r_ = r_ring.next()
                    tk = P.op("act", lambda a, o=r_.ap, i=up.ap: a.activation(out=o, in_=i, func=AF.Relu), up.rdeps() + r_.wdeps())
                    up.add_r(tk)
                    r_.set_w(tk)
                    tk = P.op("dve", lambda v, o=hT.ap[:, fcq, :], a_=r_.ap: v.tensor_tensor(out=o, in0=a_, in1=a_, op=ALU.mult), r_.rdeps() + hwd)
                    r_.add_r(tk)
                    htoks.append(tk)
                hT.set_w(*htoks)
                for t4 in range(4):
                    t = tb * 4 + t4
                    for half in range(2):
                        hs = slice(half * 512, (half + 1) * 512)
                        dn = dn_ring.next()
                        last = None
                        for kc in range(8):
                            last = P.op("pe", lambda pe, o=dn.ap, l=hT.ap[:, kc, t4 * 128:(t4 + 1) * 128], r=wd.ap[:, kc, hs], k=kc: pe.matmul(o, lhsT=l, rhs=r, start=(k == 0), stop=(k == 7)),
                                        hT.rdeps() + wd.rdeps() + dn.wdeps())
                        hT.add_r(last)
                        wd.add_r(last)
                        dn.set_w(last)
                        ff = ff_ring.next()
                        tk = P.op("act", lambda a, o=ff.ap, i=dn.ap: a.activation(out=o, in_=i, func=AF.Copy), dn.rdeps() + ff.wdeps())
                        dn.add_r(tk)
                        ff.set_w(tk)
                        tk = P.op("dve", lambda v, o=x_sb[:, t, hs], b_=ff.ap: v.tensor_tensor(out=o, in0=o, in1=b_, op=ALU.add), ff.rdeps() + xbufs[t].wdeps())
                        ff.add_r(tk)
                        xbufs[t].add_w(tk)
        out_toks = []
        for t in range(NT):
            dtk, lr = L.emit(xbufs[t].ap, xbufs[t].rdeps(), x2[t * 128:(t + 1) * 128, :])
            out_toks.append(dtk)
        P.wait_all("sp", out_toks)
        P.emit()
    return nc


_PROGS = {}


def _prog(name):
    if name not in _PROGS:
        _PROGS[name] = {"p1": build_p1, "p2": build_p2, "p3a": build_p3a, "p3b": build_p3b}[name]()
    return _PROGS[name]


def _run(name, maps):
    res = run_bass_kernel_spmd(_prog(name), maps, core_ids=list(range(NCORES)))
    return res.results


def _rep(v):
    return np.ascontiguousarray(np.broadcast_to(np.asarray(v, np.float32)[None, :], (128, v.shape[0])))


def _wbr_rows():
    rows = list(range(0, 256))
    for i in range(4):
        rows += list(range(256 + 64 * i, 256 + 64 * i + 64))
        rows += list(range(256 + 64 * (i + 4), 256 + 64 * (i + 4) + 64))
    rows += list(range(768, 1024))
    return np.array(rows)


def kernel(x, w_in, b_gate, q_norm_b, k_norm_b, rpb_c, w_branch_a, w_branch_b, w_branch_c, w_out,
           ln1_g, ln1_b, w_up, w_down, ln2_g, ln2_b):
    x = np.asarray(x, np.float32)
    xs = [np.ascontiguousarray(x[c // 4, (c % 4) * T:(c % 4 + 1) * T]) for c in range(NCORES)]
    rots = [_rot_tables((c % 4) * T) for c in range(NCORES)]
    cst1 = _p1_consts()
    ident = np.eye(128, dtype=np.float32)
    qkc, vcol, brow = _qk_cols(), _v_cols(), _wbr_rows()
    for l in range(DEPTH):
        wl = np.asarray(w_in[l], np.float32)
        wqk = np.ascontiguousarray(wl[:, qkc])
        wv = np.ascontiguousarray(wl[:, vcol])
        wg = np.ascontiguousarray(wl[:, 2304:5376])
        gains = np.ascontiguousarray(np.stack([np.tile(np.asarray(q_norm_b[l], np.float32), 2), np.tile(np.asarray(k_norm_b[l], np.float32), 2)], axis=1))
        r1 = _run("p1", [{"x": xs[c], "wqk": wqk, "wv": wv, "rot": rots[c], "cst": cst1, "gains": gains} for c in range(NCORES)])
        maps2 = _gather_p2([r["qkT"] for r in r1], [r["vtok"] for r in r1], np.asarray(rpb_c[l], np.float32))
        r2 = _run("p2", maps2)
        bg = np.ascontiguousarray(np.asarray(b_gate[l], np.float32).reshape(24, 128).T)
        wbr = np.ascontiguousarray(np.concatenate([w_branch_a[l], w_branch_b[l], w_branch_c[l]], axis=0)[brow].astype(np.float32))
        ln1 = np.stack([_rep(ln1_g[l]), _rep(ln1_b[l])])
        r3 = _run("p3a", [{"x": xs[c], "oT": r2[c]["oT"], "wg": wg, "bg": bg, "wbr": wbr, "wout": np.asarray(w_out[l], np.float32), "ln": ln1, "cst": ident} for c in range(NCORES)])
        ln2 = np.stack([_rep(ln2_g[l]), _rep(ln2_b[l])])
        r4 = _run("p3b", [{"x": r3[c]["x1"], "wup": np.asarray(w_up[l], np.float32), "wdn": np.asarray(w_down[l], np.float32), "ln": ln2, "cst": ident} for c in range(NCORES)])
        xs = [np.ascontiguousarray(r4[c]["x2"]) for c in range(NCORES)]
    out = np.zeros((2, S, D), np.float32)
    for c in range(NCORES):
        out[c // 4, (c % 4) * T:(c % 4 + 1) * T] = xs[c]
    return out
```
